# Optimizing a Trainium2 kernel written in Bass

```python
import math
import jax, jax.numpy as jnp
from jax import lax
import numpy as np

D_MODEL = 2048
BATCH = 1
SEQ = 8192
DEPTH = 4

N_META = 16
CHUNK = 128
BRANCH_WIDTH = D_MODEL // 2
RET_HEADS = 8
RET_DK = BRANCH_WIDTH // RET_HEADS
RET_DV = BRANCH_WIDTH // RET_HEADS
ROPE_BASE = 10000.0
GLA_HEADS = 4
GLA_DK = BRANCH_WIDTH // 2 // GLA_HEADS
GLA_DV = BRANCH_WIDTH // GLA_HEADS
GLA_RANK = 16
GLA_GATE_NORMALIZER = 16.0
S5_WIDTH = BRANCH_WIDTH
S5_GROUP = 16
S5_GROUPS = S5_WIDTH // S5_GROUP
S5_STATE = 64
N_BRANCH = 3
NORM_EPS = 1e-6

IN_SPLITS = (
    RET_HEADS * RET_DK,
    RET_HEADS * RET_DK,
    RET_HEADS * RET_DV,
    BRANCH_WIDTH,
    GLA_HEADS * GLA_DK,
    GLA_HEADS * GLA_DK,
    GLA_HEADS * GLA_DV,
    BRANCH_WIDTH,
    GLA_RANK,
    S5_WIDTH,
    S5_WIDTH,
    N_BRANCH * D_MODEL,
)
N_IN = sum(IN_SPLITS)

kernel_name = "hybrid_retention_gla_s5_meta"


def rmsnorm(x, g):
    xf = x.astype(jnp.float32)
    y = xf * lax.rsqrt(jnp.mean(xf * xf, axis=-1, keepdims=True) + NORM_EPS)
    return (y * g.astype(jnp.float32)).astype(x.dtype)


def rotary(t, pos):
    half = t.shape[-1] // 2
    inv = ROPE_BASE ** (-jnp.arange(half, dtype=jnp.float32) / half)
    ang = pos[:, None] * inv[None, :]
    cos = jnp.cos(ang)[None, :, None, :]
    sin = jnp.sin(ang)[None, :, None, :]
    t1, t2 = t[..., :half], t[..., half:]
    return jnp.concatenate([t1 * cos - t2 * sin, t1 * sin + t2 * cos], axis=-1)


def to_chunks(t):
    pad = CHUNK - N_META
    t = jnp.pad(t, [(0, 0), (pad, 0)] + [(0, 0)] * (t.ndim - 2))
    return t.reshape(t.shape[0], -1, CHUNK, *t.shape[2:])


def from_chunks(t):
    t = t.reshape(t.shape[0], -1, *t.shape[3:])
    return t[:, CHUNK - N_META:]


def chunk_state_scan(decay, kv):
    def step(s, inp):
        d, u = inp
        return d[..., None] * s + u, s
    s0 = jnp.zeros_like(kv[:, 0])
    _, prev = lax.scan(step, s0, (jnp.moveaxis(decay, 1, 0), jnp.moveaxis(kv, 1, 0)))
    return jnp.moveaxis(prev, 0, 1)


def retention(q, k, v):
    nh = q.shape[2]
    log_gamma = jnp.log1p(-jnp.exp2(-5.0 - jnp.arange(nh, dtype=jnp.float32)))
    idx = jnp.arange(CHUNK, dtype=jnp.float32)
    rel = idx[:, None] - idx[None, :]
    dmat = jnp.where(rel >= 0, jnp.exp(log_gamma[:, None, None] * jnp.maximum(rel, 0.0)), 0.0)
    qc, kc, vc = to_chunks(q), to_chunks(k), to_chunks(v)
    scores = jnp.einsum('bnchd,bnshd->bnhcs', qc, kc) * dmat
    out = jnp.einsum('bnhcs,bnshv->bnchv', scores, vc)
    k_dec = jnp.exp(log_gamma[:, None] * (CHUNK - 1 - idx)[None, :])
    kv = jnp.einsum('bnshd,hs,bnshv->bnhdv', kc, k_dec, vc)
    decay = jnp.broadcast_to(jnp.exp(log_gamma * CHUNK)[None, None, :, None], kv.shape[:-1])
    prev = chunk_state_scan(decay, kv)
    q_dec = jnp.exp(log_gamma[:, None] * (idx + 1.0)[None, :])
    out = out + jnp.einsum('bnchd,hc,bnhdv->bnchv', qc, q_dec, prev)
    return from_chunks(out)


def gla(q, k, v, log_a):
    qc, kc, vc, ac = to_chunks(q), to_chunks(k), to_chunks(v), to_chunks(log_a)
    b = jnp.cumsum(ac, axis=2)
    q_in = qc * jnp.exp(b)
    k_in = kc * jnp.exp(-b)
    scores = jnp.einsum('bnchd,bnshd->bnhcs', q_in, k_in)
    mask = jnp.tril(jnp.ones((CHUNK, CHUNK), dtype=bool))
    scores = jnp.where(mask, scores, 0.0)
    out = jnp.einsum('bnhcs,bnshv->bnchv', scores, vc)
    b_last = b[:, :, -1]
    kv = jnp.einsum('bnshd,bnshv->bnhdv', kc * jnp.exp(b_last[:, :, None] - b), vc)
    prev = chunk_state_scan(jnp.exp(b_last), kv)
    out = out + jnp.einsum('bnchd,bnhdv->bnchv', q_in, prev)
    return from_chunks(out)


def complex_affine_combine(e1, e2):
    a1r, a1i, b1r, b1i = e1
    a2r, a2i, b2r, b2i = e2
    return (a2r * a1r - a2i * a1i,
            a2r * a1i + a2i * a1r,
            a2r * b1r - a2i * b1i + b2r,
            a2r * b1i + a2i * b1r + b2i)


def s5_ssm(u, lam_re, lam_im, log_dt, b_re, b_im, c_re, c_im, d):
    bsz, length, _ = u.shape
    dt = jnp.exp(log_dt)[:, None]
    mag = jnp.exp(lam_re * dt)
    abar_re = mag * jnp.cos(lam_im * dt)
    abar_im = mag * jnp.sin(lam_im * dt)
    den = lam_re * lam_re + lam_im * lam_im
    nr = abar_re - 1.0
    ni = abar_im
    f_re = (nr * lam_re + ni * lam_im) / den
    f_im = (ni * lam_re - nr * lam_im) / den
    bb_re = f_re[..., None] * b_re - f_im[..., None] * b_im
    bb_im = f_re[..., None] * b_im + f_im[..., None] * b_re
    ug = u.reshape(bsz, length, S5_GROUPS, S5_GROUP)
    bu_re = jnp.einsum('blgc,gpc->blgp', ug, bb_re)
    bu_im = jnp.einsum('blgc,gpc->blgp', ug, bb_im)
    a_re = jnp.broadcast_to(abar_re, bu_re.shape)
    a_im = jnp.broadcast_to(abar_im, bu_im.shape)
    _, _, h_re, h_im = lax.associative_scan(complex_affine_combine, (a_re, a_im, bu_re, bu_im), axis=1)
    y = jnp.einsum('blgp,gcp->blgc', h_re, c_re) - jnp.einsum('blgp,gcp->blgc', h_im, c_im)
    return y.reshape(bsz, length, S5_WIDTH) + d * u


def hybrid_layer(h_res, pos, norm_g, w_in, ret_norm_g, gla_w_a, gla_b_a, gla_norm_g,
                 s5_lambda_re, s5_lambda_im, s5_log_dt, s5_b_re, s5_b_im, s5_c_re, s5_c_im, s5_d,
                 s5_w_glu, w_br_ret, w_br_gla, w_br_s5, w_out):
    f32 = jnp.float32
    dt = h_res.dtype
    bsz, length, _ = h_res.shape
    h = rmsnorm(h_res, norm_g)
    proj = h @ w_in
    offs = [int(o) for o in np.cumsum(IN_SPLITS)[:-1]]
    (rq, rk, rv, rg, gq, gk, gv, gg, glr, su, sg, mg) = jnp.split(proj, offs, axis=-1)

    rq = rotary(rq.reshape(bsz, length, RET_HEADS, RET_DK).astype(f32), pos)
    rk = rotary(rk.reshape(bsz, length, RET_HEADS, RET_DK).astype(f32), pos) * (RET_DK ** -0.5)
    ro = retention(rq, rk, rv.reshape(bsz, length, RET_HEADS, RET_DV).astype(f32))
    mu = jnp.mean(ro, axis=-1, keepdims=True)
    var = jnp.mean(jnp.square(ro - mu), axis=-1, keepdims=True)
    ro = ((ro - mu) * lax.rsqrt(var + NORM_EPS)).reshape(bsz, length, BRANCH_WIDTH) * ret_norm_g.astype(f32)
    ro = (ro * jax.nn.silu(rg.astype(f32))).astype(dt)

    log_a = jax.nn.log_sigmoid((glr @ gla_w_a + gla_b_a).astype(f32)) / GLA_GATE_NORMALIZER
    go = gla(gq.reshape(bsz, length, GLA_HEADS, GLA_DK).astype(f32) * (GLA_DK ** -0.5),
             gk.reshape(bsz, length, GLA_HEADS, GLA_DK).astype(f32),
             gv.reshape(bsz, length, GLA_HEADS, GLA_DV).astype(f32),
             log_a.reshape(bsz, length, GLA_HEADS, GLA_DK))
    go = go * lax.rsqrt(jnp.mean(go * go, axis=-1, keepdims=True) + NORM_EPS) * gla_norm_g.astype(f32)
    go = (go.reshape(bsz, length, BRANCH_WIDTH) * jax.nn.silu(gg.astype(f32))).astype(dt)

    so = s5_ssm(su.astype(f32), s5_lambda_re.astype(f32), s5_lambda_im.astype(f32), s5_log_dt.astype(f32),
                s5_b_re.astype(f32), s5_b_im.astype(f32), s5_c_re.astype(f32), s5_c_im.astype(f32),
                s5_d.astype(f32))
    so = jax.nn.gelu(so)
    so = so * jax.nn.sigmoid(so @ s5_w_glu.astype(f32))
    so = (so * jax.nn.silu(sg.astype(f32))).astype(dt)

    g_r, g_g, g_s = jnp.split(mg, N_BRANCH, axis=-1)
    y = (jax.nn.sigmoid(g_r) * (ro @ w_br_ret)
         + jax.nn.sigmoid(g_g) * (go @ w_br_gla)
         + jax.nn.sigmoid(g_s) * (so @ w_br_s5))
    return h_res + y @ w_out


def setup_inputs(seed: int = 0) -> dict:
    key = jax.random.key(seed)
    ks = jax.random.split(key, 24)
    f32 = jnp.float32
    nrm = lambda k, shape, scale: jax.random.normal(k, shape, f32) * scale
    x = nrm(ks[0], (BATCH, SEQ, D_MODEL), 1.0)
    meta = nrm(ks[1], (N_META, D_MODEL), 1.0)
    norm_g = 1.0 + nrm(ks[2], (DEPTH, D_MODEL), 0.01)
    w_in = nrm(ks[3], (DEPTH, D_MODEL, N_IN), D_MODEL ** -0.5)
    ret_norm_g = 1.0 + nrm(ks[4], (DEPTH, BRANCH_WIDTH), 0.01)
    gla_w_a = nrm(ks[5], (DEPTH, GLA_RANK, GLA_HEADS * GLA_DK), GLA_RANK ** -0.5)
    gla_b_a = nrm(ks[6], (DEPTH, GLA_HEADS * GLA_DK), 0.01)
    gla_norm_g = 1.0 + nrm(ks[7], (DEPTH, GLA_DV), 0.01)
    s5_lambda_re = -0.5 + nrm(ks[8], (DEPTH, S5_GROUPS, S5_STATE), 0.01)
    s5_lambda_im = math.pi * jnp.arange(S5_STATE, dtype=f32) + nrm(ks[9], (DEPTH, S5_GROUPS, S5_STATE), 0.01)
    s5_log_dt = jax.random.uniform(ks[10], (DEPTH, S5_GROUPS), f32, math.log(1e-3), math.log(1e-1))
    bscale = (2.0 * S5_GROUP) ** -0.5
    s5_b_re = nrm(ks[11], (DEPTH, S5_GROUPS, S5_STATE, S5_GROUP), bscale)
    s5_b_im = nrm(ks[12], (DEPTH, S5_GROUPS, S5_STATE, S5_GROUP), bscale)
    cscale = (2.0 * S5_STATE) ** -0.5
    s5_c_re = nrm(ks[13], (DEPTH, S5_GROUPS, S5_GROUP, S5_STATE), cscale)
    s5_c_im = nrm(ks[14], (DEPTH, S5_GROUPS, S5_GROUP, S5_STATE), cscale)
    s5_d = nrm(ks[15], (DEPTH, S5_WIDTH), 1.0)
    s5_w_glu = nrm(ks[16], (DEPTH, S5_WIDTH, S5_WIDTH), S5_WIDTH ** -0.5)
    w_br_ret = nrm(ks[17], (DEPTH, BRANCH_WIDTH, D_MODEL), BRANCH_WIDTH ** -0.5)
    w_br_gla = nrm(ks[18], (DEPTH, BRANCH_WIDTH, D_MODEL), BRANCH_WIDTH ** -0.5)
    w_br_s5 = nrm(ks[19], (DEPTH, S5_WIDTH, D_MODEL), S5_WIDTH ** -0.5)
    w_out = nrm(ks[20], (DEPTH, D_MODEL, D_MODEL), D_MODEL ** -0.5)
    final_g = 1.0 + nrm(ks[21], (D_MODEL,), 0.01)
    return {"x": x, "meta": meta, "norm_g": norm_g, "w_in": w_in, "ret_norm_g": ret_norm_g,
            "gla_w_a": gla_w_a, "gla_b_a": gla_b_a, "gla_norm_g": gla_norm_g,
            "s5_lambda_re": s5_lambda_re, "s5_lambda_im": s5_lambda_im, "s5_log_dt": s5_log_dt,
            "s5_b_re": s5_b_re, "s5_b_im": s5_b_im, "s5_c_re": s5_c_re, "s5_c_im": s5_c_im,
            "s5_d": s5_d, "s5_w_glu": s5_w_glu, "w_br_ret": w_br_ret, "w_br_gla": w_br_gla,
            "w_br_s5": w_br_s5, "w_out": w_out, "final_g": final_g}


def reference(x, meta, norm_g, w_in, ret_norm_g, gla_w_a, gla_b_a, gla_norm_g,
              s5_lambda_re, s5_lambda_im, s5_log_dt, s5_b_re, s5_b_im, s5_c_re, s5_c_im,
              s5_d, s5_w_glu, w_br_ret, w_br_gla, w_br_s5, w_out, final_g):
    bsz = x.shape[0]
    meta_b = jnp.broadcast_to(meta[None].astype(x.dtype), (bsz, N_META, D_MODEL))
    h = jnp.concatenate([meta_b, x], axis=1)
    pos = jnp.arange(h.shape[1], dtype=jnp.float32)
    for l in range(DEPTH):
        h = hybrid_layer(h, pos, norm_g[l], w_in[l], ret_norm_g[l], gla_w_a[l], gla_b_a[l], gla_norm_g[l],
                         s5_lambda_re[l], s5_lambda_im[l], s5_log_dt[l], s5_b_re[l], s5_b_im[l],
                         s5_c_re[l], s5_c_im[l], s5_d[l], s5_w_glu[l], w_br_ret[l], w_br_gla[l],
                         w_br_s5[l], w_out[l])
    h = rmsnorm(h, final_g)
    return h[:, N_META:]
```

```python
import contextlib
import numpy as np
import concourse.bass as bass
import concourse.mybir as mybir
from concourse.bass_utils import run_bass_kernel_spmd

F32 = mybir.dt.float32
BF16 = mybir.dt.bfloat16
I32 = mybir.dt.int32
AF = mybir.ActivationFunctionType
ALU = mybir.AluOpType
AX = mybir.AxisListType

D = 2048
NCORE = 8
NCH = 9
C = 128
NT = NCH * C
BW = 1024
N_IN = 15376
EPS = 1e-6
O_RQ, O_RK, O_RV, O_RG = 0, 1024, 2048, 3072
O_GQ, O_GK, O_GV, O_GG, O_GLR = 4096, 4608, 5120, 6144, 7168
O_SU, O_SG, O_MG = 7184, 8208, 9232


class V:
    def __init__(self, ap, tile):
        self.ap, self.tile = ap, tile

    def __getitem__(self, idx):
        return V(self.ap[idx], self.tile)

    def bc(self, shape):
        return V(self.ap.broadcast_to(shape), self.tile)

    def re(self, s, **kw):
        return V(self.ap.rearrange(s, **kw), self.tile)

    def pb(self, n):
        return V(self.ap.partition_broadcast(n), self.tile)


class Tile:
    def __init__(self, handle, name):
        self.h, self.name = handle, name
        self.last_w = None
        self.readers = {}
        self.psum = False

    def __getitem__(self, idx):
        return V(self.h[idx], self)

    @property
    def v(self):
        return V(self.h[:], self)


class K:
    def __init__(self, nc, es):
        self.nc, self.es = nc, es
        self.eng = {"pe": nc.tensor, "dve": nc.vector, "act": nc.scalar, "pool": nc.gpsimd, "sp": nc.sync}
        self.sem = {k: es.enter_context(nc.semaphore("sem_" + k)) for k in ["pe", "dve", "act", "pool"]}
        self.cnt = {k: 0 for k in self.sem}
        self.dsem = [es.enter_context(nc.semaphore(f"dsem{i}")) for i in range(10)]
        self.dcnt = [0] * len(self.dsem)
        self.dnext = 0
        self.waited = {k: {} for k in self.eng}
        self.n = 0

    def sb(self, name, shape, dt, stack=None):
        self.n += 1
        h = (stack or self.es).enter_context(self.nc.sbuf_tensor(f"{name}_{self.n}", list(shape), dt))
        return Tile(h, name)

    def ps(self, name, shape, dt, stack=None):
        self.n += 1
        h = (stack or self.es).enter_context(self.nc.psum_tensor(f"{name}_{self.n}", list(shape), dt))
        t = Tile(h, name)
        t.psum = True
        return t

    def dram(self, name, shape, dt, kind):
        return Tile(self.nc.dram_tensor(name, list(shape), dt, kind=kind).ap(), name)

    def _wait(self, ek, reads, writes):
        deps = {}
        for r in reads:
            if r.last_w:
                deps[r.last_w[0]] = max(deps.get(r.last_w[0], (None, 0)), r.last_w, key=lambda t: t[1])
            if r.psum:
                for tok in r.readers.values():
                    if tok[0] != ("e", ek):
                        deps[tok[0]] = max(deps.get(tok[0], (None, 0)), tok, key=lambda t: t[1])
        for w in writes:
            for tok in ([w.last_w] if w.last_w else []) + list(w.readers.values()):
                deps[tok[0]] = max(deps.get(tok[0], (None, 0)), tok, key=lambda t: t[1])
        for key, (sk, val) in deps.items():
            if ek == "pe" and sk == ("e", "pe"):
                continue
            if self.waited[ek].get(sk, 0) < val:
                sem = self.sem[sk[1]] if sk[0] == "e" else self.dsem[sk[1]]
                self.eng[ek].wait_ge(sem, val)
                self.waited[ek][sk] = val

    def op(self, ek, name, **kw):
        reads, writes, args = [], [], {}
        for k_, v_ in kw.items():
            if isinstance(v_, V):
                (writes if k_ in ("out", "accum_out", "ap") else reads).append(v_.tile)
                args[k_] = v_.ap
            else:
                args[k_] = v_
        self._wait(ek, reads, writes)
        ins = getattr(self.eng[ek], name)(**args)
        self.cnt[ek] += 1
        ins.then_inc(self.sem[ek], 1)
        tok = (("e", ek), self.cnt[ek])
        for r in reads:
            r.readers[("e", ek)] = tok
        for w in writes:
            w.last_w, w.readers = tok, {}
        return ins

    def dma(self, out, in_, q="sp", **kw):
        self._wait(q, [in_.tile], [out.tile])
        i = self.dnext
        self.dnext = (self.dnext + 1) % len(self.dsem)
        sk = ("d", i)
        if self.waited[q].get(sk, 0) < self.dcnt[i]:
            self.eng[q].wait_ge(self.dsem[i], self.dcnt[i])
            self.waited[q][sk] = self.dcnt[i]
        self.eng[q].dma_start(out=out.ap, in_=in_.ap, **kw).then_inc(self.dsem[i], 16)
        self.dcnt[i] += 16
        tok = (sk, self.dcnt[i])
        in_.tile.readers[sk] = tok
        out.tile.last_w, out.tile.readers = tok, {}

    @contextlib.contextmanager
    def scope(self):
        with contextlib.ExitStack() as st:
            yield st
        self.barrier()

    def barrier(self):
        for ek in ("pe", "dve", "act", "sp"):
            for k2 in ("pe", "dve", "act"):
                sk = ("e", k2)
                if k2 != ek and self.waited[ek].get(sk, 0) < self.cnt[k2]:
                    self.eng[ek].wait_ge(self.sem[k2], self.cnt[k2])
                    self.waited[ek][sk] = self.cnt[k2]
            for i in range(len(self.dsem)):
                sk = ("d", i)
                if self.waited[ek].get(sk, 0) < self.dcnt[i]:
                    self.eng[ek].wait_ge(self.dsem[i], self.dcnt[i])
                    self.waited[ek][sk] = self.dcnt[i]

    def finish(self, outs):
        self._wait("sp", [o for o in outs], [])

    def mm(self, out, lhsT, rhs, start=True, stop=True):
        return self.op("pe", "matmul", out=out, lhsT=lhsT, rhs=rhs, start=start, stop=stop)

    def tr(self, out, in_, ident):
        return self.op("pe", "transpose", out=out, in_=in_, identity=ident)

    def act(self, out, in_, func, **kw):
        return self.op("act", "activation", out=out, in_=in_, func=func, **kw)

    def tt(self, out, in0, in1, op, e="dve"):
        return self.op(e, "tensor_tensor", out=out, in0=in0, in1=in1, op=op)

    def ts(self, out, in0, s1, op0, s2=None, op1=None, e="dve", **kw):
        if op1 is None:
            return self.op(e, "tensor_scalar", out=out, in0=in0, scalar1=s1, scalar2=None, op0=op0, **kw)
        return self.op(e, "tensor_scalar", out=out, in0=in0, scalar1=s1, scalar2=s2, op0=op0, op1=op1, **kw)

    def stt(self, out, in0, scalar, in1, op0, op1):
        return self.op("dve", "scalar_tensor_tensor", out=out, in0=in0, scalar=scalar, in1=in1, op0=op0, op1=op1)

    def cp(self, out, in_, e="dve"):
        if e == "act":
            return self.act(out, in_, AF.Identity)
        return self.op(e, "tensor_copy", out=out, in_=in_)

    def memset(self, ap, val, e="dve"):
        return self.op(e, "memset", ap=ap, constant=val)


TWO_PI = 6.283185
INV_2PI = 1.0 / 6.283185307179586


class Prog:
    def __init__(self, mode, final):
        self.mode, self.final = mode, final
        self.nc = bass.Bass("TRN2", target_bir_lowering=False)
        self.es = contextlib.ExitStack()
        self.k = K(self.nc, self.es)
        self.inputs = {}
        self.outputs = {}

    def inp(self, name, shape, dt=F32):
        t = self.k.dram(name, shape, dt, "ExternalInput")
        self.inputs[name] = t
        return t

    def outp(self, name, shape, dt=F32):
        t = self.k.dram(name, shape, dt, "ExternalOutput")
        self.outputs[name] = t
        return t

    A_COLS = ((O_RK, 1024), (O_RV, 1024), (O_GK, 512), (O_GV, 1024), (O_GLR, 16), (O_SU, 1024))

    def o(self, off):
        if self.mode == "B":
            return off
        acc = 0
        for base, n in self.A_COLS:
            if base == off:
                return acc
            acc += n
        raise KeyError(off)

    def load_w(self, dst, w, col0, ncols, nk=16, row0=0, dcol=0):
        k = self.k
        for kk in range(nk):
            st = self.stage[self.sti % len(self.stage)]
            self.sti += 1
            k.dma(st[:, :ncols], w[row0 + kk * 128: row0 + (kk + 1) * 128, col0:col0 + ncols])
            e = "dve" if self.sti % 2 else "act"
            k.cp(dst[:, kk, dcol:dcol + ncols], st[:, :ncols], e=e)

    def sincos(self, out, x, n, shift, tmp):
        k = self.k
        y, yi, m = tmp
        k.ts(y[:, :n], x, INV_2PI, ALU.mult, float(shift), ALU.add)
        k.cp(yi[:, :n], y[:, :n])
        k.cp(m[:, :n], yi[:, :n])
        k.tt(y[:, :n], y[:, :n], m[:, :n], ALU.subtract)
        k.ts(m[:, :n], y[:, :n], 0.5, ALU.is_gt)
        k.tt(y[:, :n], y[:, :n], m[:, :n], ALU.subtract)
        k.ts(m[:, :n], y[:, :n], -0.5, ALU.is_lt)
        k.tt(y[:, :n], y[:, :n], m[:, :n], ALU.add)
        k.act(out, y[:, :n], AF.Sin, scale=TWO_PI)

    def build_final(self):
        k = self.k
        h_d = self.inp("h", [NT, D])
        fg_d = self.inp("final_g", [128, D])
        hout_d = self.outp("h_out", [NT, D])
        ht = [k.sb(f"htf{i}", [128, D], F32) for i in range(2)]
        junk = k.sb("junkf", [128, D], F32)
        fg = k.sb("fg", [128, D], F32)
        ss = k.sb("ssf", [128, 2], F32)
        k.dma(fg.v, fg_d.v)
        for c in range(NCH):
            t = ht[c % 2]
            k.dma(t.v, h_d[c * C:(c + 1) * C, :])
            k.act(junk.v, t.v, AF.Square, accum_out=ss[:, 0:1])
            k.act(ss[:, 1:2], ss[:, 0:1], AF.Sqrt, scale=1.0 / D, bias=EPS)
            k.op("dve", "reciprocal", out=ss[:, 1:2], in_=ss[:, 1:2])
            k.act(t.v, t.v, AF.Identity, scale=ss[:, 1:2])
            k.tt(t.v, t.v, fg.v, ALU.mult)
            k.dma(hout_d[c * C:(c + 1) * C, :], t.v)
        k.finish([hout_d])
        self.es.close()
        return self.nc

    def build(self):
        if self.mode == "F":
            return self.build_final()
        k, es = self.k, self.es
        B = self.mode == "B"
        c0 = 0 if B else 1
        h_d = self.inp("h", [NT, D])
        ng_d = self.inp("norm_gT", [128, 16])
        win = self.inp("w_in", [D, N_IN if B else 4624])
        cq_d = self.inp("rot_q", [128, NCH, 2, 64])
        ck_d = self.inp("rot_k", [128, NCH, 2, 64])
        dmat_d = self.inp("dmatT", [128, 8, 128])
        kdec_d = self.inp("kdec", [128, 8])
        qdec_d = self.inp("qdecB", [128, 8, 128])
        tri_d = self.inp("tri", [128, 3, 128])
        idf_d = self.inp("identF", [128, 128])
        msk_d = self.inp("masks", [128, 16])
        wa_d = self.inp("gla_w_a", [16, 512])
        ba_d = self.inp("gla_b_a", [128, 512])
        s5p_d = self.inp("s5_P", [128, 3, 32])
        s5f_d = self.inp("s5_F", [128, 3, 4096])
        zb_d = self.inp("s5_LB", [128, 2, 32, 128])
        iota_d = self.inp("iota", [128, 128])
        if B:
            zc_d = self.inp("s5_LC", [128, 2, 32, 128])
            rng_d = self.inp("ret_norm_g", [128, 1024])
            gng_d = self.inp("gla_norm_g", [128, 256])
            s5d_d = self.inp("s5_dT", [128, 8])
            wglu = self.inp("s5_w_glu", [1024, 1024])
            wbr = [self.inp(n, [1024, D]) for n in ("w_br_ret", "w_br_gla", "w_br_s5")]
            wout = self.inp("w_out", [D, D])
            e_ret_d = self.inp("E_ret", [128, NCORE, 1024])
            e_gla_d = self.inp("E_gla", [128, NCORE, 1024])
            d_gla_d = self.inp("D_gla", [128, NCORE, 4])
            e_s5_d = self.inp("E_s5", [128, NCORE, 64])
            hout_d = self.outp("h_out", [NT, D])
            if self.final:
                fg_d = self.inp("final_g", [128, D])
        else:
            o_ret = self.outp("E_ret", [128, 1024])
            o_gla = self.outp("E_gla", [128, 1024])
            o_dgl = self.outp("D_gla", [128, 4])
            o_s5 = self.outp("E_s5", [128, 64])
            import os
            dbg_d = self.outp("dbg", [128, 12, 128]) if os.environ.get("KDBG") else None

        hnT = k.sb("hnT", [128, 16, NT], BF16)
        identF = k.sb("identF", [128, 128], F32)
        identB = k.sb("identB", [128, 128], BF16)
        masks = k.sb("masks", [128, 16], F32)
        tri = k.sb("tri", [128, 3, 128], F32)
        ngT = k.sb("ngT", [128, 16], F32)
        self.stage = [k.sb(f"stage{i}", [128, 1024], F32) for i in range(2)]
        self.sti = 0
        k.dma(identF.v, idf_d.v)
        k.cp(identB.v, identF.v)
        k.dma(masks.v, msk_d.v)
        k.dma(tri.v, tri_d.v)
        k.dma(ngT.v, ng_d.v)
        pA = [k.ps(f"pA{i}", [128, 512], F32) for i in range(2)]
        pB = [k.ps(f"pB{i}", [128, 512], F32) for i in range(4)]
        pT = [k.ps(f"pT{i}", [128, 1024], BF16) for i in range(2)]

        with k.scope() as ph:
            ht = [k.sb(f"ht{i}", [128, D], F32, ph) for i in range(2)]
            junk = k.sb("junk", [128, D], F32, ph)
            ss = k.sb("ss", [128, 2], F32, ph)
            for c in range(c0, NCH):
                t = ht[c % 2]
                k.dma(t.v, h_d[c * C:(c + 1) * C, :])
                k.act(junk.v, t.v, AF.Square, accum_out=ss[:, 0:1])
                k.act(ss[:, 1:2], ss[:, 0:1], AF.Sqrt, scale=1.0 / D, bias=EPS)
                k.op("dve", "reciprocal", out=ss[:, 1:2], in_=ss[:, 1:2])
                k.act(t.v, t.v, AF.Identity, scale=ss[:, 1:2])
                for kt in range(16):
                    p = pB[kt % 4]
                    k.tr(p[:, 0:128], t[:, kt * 128:(kt + 1) * 128], identF.v)
                    if kt % 2:
                        k.ts(hnT[:, kt, c * C:(c + 1) * C], p[:, 0:128], ngT[:, kt:kt + 1], ALU.mult)
                    else:
                        k.act(hnT[:, kt, c * C:(c + 1) * C], p[:, 0:128], AF.Copy, scale=ngT[:, kt:kt + 1])

        self.hnT, self.pA, self.pB, self.pT = hnT, pA, pB, pT
        self.identB, self.identF, self.masks, self.tri = identB, identF, masks, tri
        self.c0, self.win = c0, win
        outs = []
        import os
        stop = os.environ.get("KSTOP", "")
        soT = roT = goT = None
        if stop != "norm":
            soT = self.phase_s5(locals())
        if stop not in ("norm", "s5"):
            roT = self.phase_ret(locals())
        if stop not in ("norm", "s5", "ret"):
            goT = self.phase_gla(locals())
        if stop:
            k.finish([])
            self.es.close()
            return self.nc
        if B:
            self.phase_merge(locals(), roT, goT, soT)
            outs = [hout_d]
        else:
            outs = [o_ret, o_gla, o_dgl, o_s5]
        k.finish(outs)
        self.es.close()
        return self.nc

    def phase_ret(self, L):
        k, B, hnT, pA, pB, pT, c0 = self.k, self.mode == "B", self.hnT, self.pA, self.pB, self.pT, self.c0
        identB, masks, win = self.identB, self.masks, self.win
        roT = k.sb("roT", [128, 8, NT], BF16) if B else None
        with k.scope() as ph:
            rq = k.sb("rq", [128, NCH, 2, 64], F32, ph)
            rk = k.sb("rk", [128, NCH, 2, 64], F32, ph)
            kdec = k.sb("kdec", [128, 8], F32, ph)
            k.dma(rk.v, L["ck_d"].v)
            k.dma(kdec.v, L["kdec_d"].v)
            W = [k.sb(f"Wr{i}", [128, 16, 512], BF16, ph) for i in range(2)]
            S = k.sb("S", [128, 128], F32, ph)
            Sb = k.sb("Sb", [128, 128], BF16, ph)
            kr = k.sb("kr", [128, 128], F32, ph)
            t1 = k.sb("t1", [128, 64], F32, ph)
            t2 = k.sb("t2", [128, 64], F32, ph)
            kd = k.sb("kd", [128, 128], BF16, ph)
            vb = k.sb("vb", [128, 128], BF16, ph)
            if B:
                k.dma(rq.v, L["cq_d"].v)
                dmat = k.sb("dmat", [128, 8, 128], F32, ph)
                qdecB = k.sb("qdecB", [128, 8, 128], F32, ph)
                rng = k.sb("rng", [128, 1024], F32, ph)
                k.dma(dmat.v, L["dmat_d"].v)
                k.dma(qdecB.v, L["qdec_d"].v)
                k.dma(rng.v, L["rng_d"].v)
                Eh = k.sb("Eh", [128, NCORE, 128], F32, ph)
                Sin = k.sb("Sin", [128, 128], F32, ph)
                qr = k.sb("qr", [128, 128], F32, ph)
                qb = k.sb("qb", [128, 128], BF16, ph)
                kbf = k.sb("kbf", [128, 128], BF16, ph)
                qTs = k.sb("qTs", [128, 128], BF16, ph)
                qTd = k.sb("qTd", [128, 128], BF16, ph)
                kTs = k.sb("kTs", [128, 128], BF16, ph)
                sT = k.sb("sT", [128, 128], BF16, ph)
                st6 = k.sb("st6", [128, 6], F32, ph)
                mv = k.sb("mv", [128, 4], F32, ph)
                ro = k.sb("ro", [128, 128], F32, ph)
                sg = k.sb("sg", [128, 128], F32, ph)
                rob = k.sb("rob", [128, 128], BF16, ph)
            else:
                Eout = k.sb("Eout", [128, 1024], F32, ph)

            def rotary(dst, p, off, tab, c):
                x1, x2 = p[:, off:off + 64], p[:, off + 64:off + 128]
                cs, sn = tab[:, c, 0, :], tab[:, c, 1, :]
                k.tt(t1.v, x1, cs, ALU.mult)
                k.tt(t2.v, x2, sn, ALU.mult)
                k.tt(dst[:, 0:64], t1.v, t2.v, ALU.subtract)
                k.tt(t1.v, x1, sn, ALU.mult)
                k.tt(t2.v, x2, cs, ALU.mult)
                k.tt(dst[:, 64:128], t1.v, t2.v, ALU.add)

            for hh in range(8):
                gam = 1.0 - 2.0 ** (-5.0 - hh)
                Wt = W[hh % 2]
                if B:
                    self.load_w(Wt, win, O_RQ + 128 * hh, 128, dcol=0)
                    self.load_w(Wt, win, O_RG + 128 * hh, 128, dcol=384)
                self.load_w(Wt, win, self.o(O_RK) + 128 * hh, 128, dcol=128)
                self.load_w(Wt, win, self.o(O_RV) + 128 * hh, 128, dcol=256)
                k.memset(S.v, 0.0)
                k.memset(Sb.v, 0.0)
                for c in range(c0, NCH):
                    p = pA[c % 2]
                    lo, hi = (0, 512) if B else (128, 384)
                    for kt in range(16):
                        k.mm(p[:, lo:hi], hnT[:, kt, c * C:(c + 1) * C], Wt[:, kt, lo:hi], start=kt == 0, stop=kt == 15)
                    rotary(kr, p, 128, rk, c)
                    k.ts(kd.v, kr.v, kdec[:, hh:hh + 1], ALU.mult)
                    k.act(vb.v, p[:, 256:384], AF.Identity)
                    if B:
                        rotary(qr, p, 0, rq, c)
                        k.cp(qb.v, qr.v)
                        k.cp(kbf.v, kr.v)
                        k.tr(pT[0][:, 0:128], qb.v, identB.v)
                        k.tr(pT[0][:, 128:256], kbf.v, identB.v)
                        k.act(qTs.v, pT[0][:, 0:128], AF.Identity)
                        k.tt(qTd.v, pT[0][:, 0:128], qdecB[:, hh, :], ALU.mult)
                        k.cp(kTs.v, pT[0][:, 128:256])
                        k.mm(pB[1][:, 0:128], kTs.v, qTs.v)
                        k.tt(sT.v, pB[1][:, 0:128], dmat[:, hh, :], ALU.mult)
                        k.mm(pB[2][:, 0:128], sT.v, vb.v, start=True, stop=False)
                        k.mm(pB[2][:, 0:128], qTd.v, Sb.v, start=False, stop=True)
                    k.mm(pB[0][:, 0:128], kd.v, vb.v)
                    k.stt(S.v, S.v, float(gam ** 128), pB[0][:, 0:128], ALU.mult, ALU.add)
                    if B and c == 0:
                        k.dma(Eh.v, L["e_ret_d"][:, :, hh * 128:(hh + 1) * 128])
                        k.memset(Sin.v, 0.0)
                        for cc in range(NCORE):
                            k.stt(Sin.v, S.v, masks[:, 1 + cc:2 + cc], Sin.v, ALU.mult, ALU.add)
                            k.stt(S.v, S.v, float(gam ** 1024), Eh[:, cc, :], ALU.mult, ALU.add)
                        k.cp(S.v, Sin.v)
                    if B:
                        k.cp(Sb.v, S.v)
                        o = pB[2][:, 0:128]
                        k.op("dve", "bn_stats", out=st6.v, in_=o)
                        k.op("dve", "bn_aggr", out=mv[:, 0:2], in_=st6.v)
                        k.act(mv[:, 2:3], mv[:, 1:2], AF.Sqrt, bias=EPS)
                        k.op("dve", "reciprocal", out=mv[:, 2:3], in_=mv[:, 2:3])
                        k.ts(ro.v, o, mv[:, 0:1], ALU.subtract, mv[:, 2:3], ALU.mult)
                        k.tt(ro.v, ro.v, rng[:, hh * 128:(hh + 1) * 128], ALU.mult)
                        k.act(sg.v, p[:, 384:512], AF.Silu)
                        k.tt(rob.v, ro.v, sg.v, ALU.mult)
                        k.tr(pT[1][:, 0:128], rob.v, identB.v)
                        k.act(roT[:, hh, c * C:(c + 1) * C], pT[1][:, 0:128], AF.Identity)
                if not B:
                    k.cp(Eout[:, hh * 128:(hh + 1) * 128], S.v)
            if not B:
                k.dma(L["o_ret"].v, Eout.v)
        return roT

    def phase_gla(self, L):
        k, B, hnT, pA, pB, pT, c0 = self.k, self.mode == "B", self.hnT, self.pA, self.pB, self.pT, self.c0
        identB, masks, win, tri = self.identB, self.masks, self.win, self.tri
        goT = k.sb("goT", [128, 8, NT], BF16) if B else None
        with k.scope() as ph:
            bsb = k.sb("bsb", [128, NCH, 512], F32, ph)
            rbsb = k.sb("rbsb", [128, NCH, 512], F32, ph)
            dls = k.sb("dls", [128, NCH, 4], F32, ph)
            with k.scope() as p2:
                Wg = k.sb("Wglr", [128, 16, 16], BF16, p2)
                self.load_w(Wg, win, self.o(O_GLR), 16)
                waf = k.sb("waf", [16, 512], F32, p2)
                wab = k.sb("wab", [16, 512], BF16, p2)
                bab = k.sb("bab", [128, 512], F32, p2)
                k.dma(waf.v, L["wa_d"].v)
                k.cp(wab.v, waf.v)
                k.dma(bab.v, L["ba_d"].v)
                glr = k.sb("glr", [128, 16], BF16, p2)
                glrT = k.sb("glrT", [16, 128], BF16, p2)
                xs = k.sb("xs", [128, 512], F32, p2)
                la = k.sb("la", [128, 512], F32, p2)
                for c in range(c0, NCH):
                    p = pA[c % 2]
                    for kt in range(16):
                        k.mm(p[:, 0:16], hnT[:, kt, c * C:(c + 1) * C], Wg[:, kt, :], start=kt == 0, stop=kt == 15)
                    k.cp(glr.v, p[:, 0:16])
                    k.tr(pT[0][0:16, 0:128], glr.v, identB.v)
                    k.cp(glrT.v, pT[0][0:16, 0:128])
                    k.mm(pB[0][:, 0:512], glrT.v, wab.v)
                    k.tt(xs.v, pB[0][:, 0:512], bab.v, ALU.add)
                    k.act(xs.v, xs.v, AF.Exp, scale=-1.0)
                    k.act(xs.v, xs.v, AF.Ln, bias=1.0)
                    k.ts(la.v, xs.v, -1.0 / 16.0, ALU.mult)
                    if c == 0:
                        k.ts(la.v, la.v, masks[:, 0:1], ALU.mult)
                    k.mm(pB[1][:, 0:512], tri[:, 0, :], la.v)
                    k.mm(pB[2][:, 0:512], tri[:, 1, :], la.v)
                    k.act(bsb[:, c, :], pB[1][:, 0:512], AF.Identity)
                    k.cp(rbsb[:, c, :], pB[2][:, 0:512])
                    for hh in range(4):
                        k.mm(pB[3][:, 2 * hh:2 * hh + 2], la[:, hh * 128:(hh + 1) * 128], tri[:, 2, 0:2])
                    k.act(dls[:, c, :], pB[3][:, 0:8:2], AF.Exp)
            W = [k.sb(f"Wg{i}", [128, 16, 768], BF16, ph) for i in range(1)]
            S = k.sb("S", [128, 256], F32, ph)
            Sb = k.sb("Sb", [128, 256], BF16, ph)
            ex = k.sb("ex", [128, 128], F32, ph)
            kbb = k.sb("kbb", [128, 128], BF16, ph)
            vb = k.sb("vb", [128, 256], BF16, ph)
            if B:
                gng = k.sb("gng", [128, 256], F32, ph)
                k.dma(gng.v, L["gng_d"].v)
                Eh = k.sb("Eh", [128, NCORE, 256], F32, ph)
                Dh = k.sb("Dh", [128, NCORE, 4], F32, ph)
                k.dma(Dh.v, L["d_gla_d"].v)
                Sin = k.sb("Sin", [128, 256], F32, ph)
                qinb = k.sb("qinb", [128, 128], BF16, ph)
                kinb = k.sb("kinb", [128, 128], BF16, ph)
                qT = k.sb("qT", [128, 128], BF16, ph)
                kT = k.sb("kT", [128, 128], BF16, ph)
                sT = k.sb("sT", [128, 128], BF16, ph)
                junk = k.sb("junk", [128, 256], F32, ph)
                ss = k.sb("ss", [128, 2], F32, ph)
                go = k.sb("go", [128, 256], F32, ph)
                sg = k.sb("sg", [128, 256], F32, ph)
                gob = k.sb("gob", [128, 256], BF16, ph)
            else:
                Eout = k.sb("Eout", [128, 1024], F32, ph)
                Dt = k.sb("Dt", [128, 4], F32, ph)
                k.memset(Dt.v, 1.0)
            for hh in range(4):
                Wt = W[0]
                if B:
                    self.load_w(Wt, win, O_GQ + 128 * hh, 128, dcol=0)
                    self.load_w(Wt, win, O_GG + 256 * hh, 256, dcol=512)
                self.load_w(Wt, win, self.o(O_GK) + 128 * hh, 128, dcol=128)
                self.load_w(Wt, win, self.o(O_GV) + 256 * hh, 256, dcol=256)
                k.memset(S.v, 0.0)
                k.memset(Sb.v, 0.0)
                hs = slice(hh * 128, (hh + 1) * 128)
                for c in range(c0, NCH):
                    p0, p1 = pA[0], pA[1]
                    lo = 0 if B else 128
                    for kt in range(16):
                        k.mm(p0[:, lo:512], hnT[:, kt, c * C:(c + 1) * C], Wt[:, kt, lo:512], start=kt == 0, stop=kt == 15)
                    if B:
                        for kt in range(16):
                            k.mm(p1[:, 0:256], hnT[:, kt, c * C:(c + 1) * C], Wt[:, kt, 512:768], start=kt == 0, stop=kt == 15)
                    k.act(ex.v, rbsb[:, c, hs], AF.Exp)
                    k.tt(kbb.v, p0[:, 128:256], ex.v, ALU.mult)
                    k.act(vb.v, p0[:, 256:512], AF.Identity)
                    if B:
                        k.act(ex.v, bsb[:, c, hs], AF.Exp, scale=-1.0)
                        k.tt(kinb.v, p0[:, 128:256], ex.v, ALU.mult)
                        k.act(ex.v, bsb[:, c, hs], AF.Exp)
                        k.stt(qinb.v, p0[:, 0:128], float(128 ** -0.5), ex.v, ALU.mult, ALU.mult)
                        k.tr(pT[0][:, 0:128], qinb.v, identB.v)
                        k.tr(pT[0][:, 128:256], kinb.v, identB.v)
                        k.act(qT.v, pT[0][:, 0:128], AF.Identity)
                        k.cp(kT.v, pT[0][:, 128:256])
                        k.mm(pB[0][:, 0:128], kT.v, qT.v)
                        k.tt(sT.v, pB[0][:, 0:128], tri[:, 0, :], ALU.mult)
                        k.mm(pB[1][:, 0:256], sT.v, vb.v, start=True, stop=False)
                        k.mm(pB[1][:, 0:256], qT.v, Sb.v, start=False, stop=True)
                    k.mm(pB[2][:, 0:256], kbb.v, vb.v)
                    k.stt(S.v, S.v, dls[:, c, hh:hh + 1], pB[2][:, 0:256], ALU.mult, ALU.add)
                    if not B:
                        k.tt(Dt[:, hh:hh + 1], Dt[:, hh:hh + 1], dls[:, c, hh:hh + 1], ALU.mult)
                    if B and c == 0:
                        k.dma(Eh.v, L["e_gla_d"][:, :, hh * 256:(hh + 1) * 256])
                        k.memset(Sin.v, 0.0)
                        for cc in range(NCORE):
                            k.stt(Sin.v, S.v, masks[:, 1 + cc:2 + cc], Sin.v, ALU.mult, ALU.add)
                            k.stt(S.v, S.v, Dh[:, cc, hh:hh + 1], Eh[:, cc, :], ALU.mult, ALU.add)
                        k.cp(S.v, Sin.v)
                    if B:
                        k.cp(Sb.v, S.v)
                        o = pB[1][:, 0:256]
                        k.act(junk.v, o, AF.Square, accum_out=ss[:, 0:1])
                        k.act(ss[:, 1:2], ss[:, 0:1], AF.Sqrt, scale=1.0 / 256, bias=EPS)
                        k.op("dve", "reciprocal", out=ss[:, 1:2], in_=ss[:, 1:2])
                        k.ts(go.v, o, ss[:, 1:2], ALU.mult)
                        k.tt(go.v, go.v, gng.v, ALU.mult)
                        k.act(sg.v, p1[:, 0:256], AF.Silu)
                        k.tt(gob.v, go.v, sg.v, ALU.mult)
                        for j in range(2):
                            k.tr(pT[1][:, j * 128:(j + 1) * 128], gob[:, j * 128:(j + 1) * 128], identB.v)
                            k.act(goT[:, 2 * hh + j, c * C:(c + 1) * C], pT[1][:, j * 128:(j + 1) * 128], AF.Identity)
                if not B:
                    k.cp(Eout[:, hh * 256:(hh + 1) * 256], S.v)
            if not B:
                k.dma(L["o_gla"].v, Eout.v)
                k.dma(L["o_dgl"].v, Dt.v)
        return goT

    def phase_s5(self, L):
        k, B, hnT, pA, pB, pT, c0 = self.k, self.mode == "B", self.hnT, self.pA, self.pB, self.pT, self.c0
        masks, win = self.masks, self.win
        soT = k.sb("soT", [128, 8, NT], BF16) if B else None
        with k.scope() as ph:
            cosT = k.sb("cosT", [128, 32, 128], F32, ph)
            sinT = k.sb("sinT", [128, 32, 128], F32, ph)
            LBre = k.sb("LBre", [128, 32, 128], BF16, ph)
            LBim = k.sb("LBim", [128, 32, 128], BF16, ph)
            sp = k.sb("s5p", [128, 3, 32], F32, ph)
            r = k.sb("r", [128, 32], F32, ph)
            th = k.sb("th", [128, 32], F32, ph)
            rot = k.sb("rot", [128, 4, 32], F32, ph)
            rfill = k.sb("rfill", [128, 32, 128], F32, ph)
            if B:
                LCre = k.sb("LCre", [128, 32, 128], BF16, ph)
                LCim = k.sb("LCim", [128, 32, 128], BF16, ph)
                geluT = k.sb("geluT", [128, 8, NT], BF16, ph)
                s5d = k.sb("s5d", [128, 8], F32, ph)
                k.dma(s5d.v, L["s5d_d"].v)
            with k.scope() as p2:
                iota = k.sb("iota", [128, 128], F32, p2)
                k.dma(iota.v, L["iota_d"].v)
                k.dma(sp.v, L["s5p_d"].v)
                tmp = (k.sb("ty", [128, 1024], F32, p2), k.sb("tyi", [128, 1024], I32, p2), k.sb("tm", [128, 1024], F32, p2))
                ang = k.sb("ang", [128, 8, 128], F32, p2)
                dt = k.sb("dt", [128, 32], F32, p2)
                lr = k.sb("lr", [128, 32], F32, p2)
                t32 = k.sb("t32", [128, 32], F32, p2)
                k.act(dt.v, sp[:, 2, :], AF.Exp)
                k.tt(lr.v, sp[:, 0, :], dt.v, ALU.mult)
                k.tt(th.v, sp[:, 1, :], dt.v, ALU.mult)
                k.act(r.v, lr.v, AF.Exp)
                for s in range(4):
                    k.tt(rfill[:, s * 8:(s + 1) * 8, :], r[:, s * 8:(s + 1) * 8].re("p (t o) -> p t o", o=1).bc([128, 8, 128]),
                         self.tri[:, 2, :].re("p (o n) -> p o n", o=1).bc([128, 8, 128]), ALU.mult)
                    k.tt(ang.v, th[:, s * 8:(s + 1) * 8].re("p (t o) -> p t o", o=1).bc([128, 8, 128]),
                         iota.v.re("p (o n) -> p o n", o=1).bc([128, 8, 128]), ALU.mult)
                    af = ang.v.re("p t n -> p (t n)")
                    self.sincos(sinT[:, s * 8:(s + 1) * 8, :].re("p t n -> p (t n)"), af, 1024, 0.0, tmp)
                    self.sincos(cosT[:, s * 8:(s + 1) * 8, :].re("p t n -> p (t n)"), af, 1024, 0.25, tmp)
                k.ts(t32.v, th.v, 128.0, ALU.mult)
                self.sincos(rot[:, 0, :], t32.v, 32, 0.25, tmp)
                self.sincos(rot[:, 1, :], t32.v, 32, 0.0, tmp)
                k.ts(t32.v, th.v, 1024.0, ALU.mult)
                self.sincos(rot[:, 2, :], t32.v, 32, 0.25, tmp)
                self.sincos(rot[:, 3, :], t32.v, 32, 0.0, tmp)
                k.act(t32.v, lr.v, AF.Exp, scale=1024.0)
                k.tt(rot[:, 2, :], rot[:, 2, :], t32.v, ALU.mult)
                k.tt(rot[:, 3, :], rot[:, 3, :], t32.v, ALU.mult)
            import os
            if os.environ.get("KS5") == "p1":
                return soT
            with k.scope() as p2:
                N = 512
                fl = [k.sb(f"fl{i}", [128, N], F32, p2) for i in range(3)]
                a = [k.sb(f"fa{i}", [128, N], F32, p2) for i in range(6)]
                tmp = (k.sb("ty", [128, N], F32, p2), k.sb("tyi", [128, N], I32, p2), k.sb("tm", [128, N], F32, p2))
                zb = [k.sb(f"zb{i}", [128, 4, 128], F32, p2) for i in range(2)]
                for s in range(8):
                    cs = slice(s * N, (s + 1) * N)
                    for i in range(3):
                        k.dma(fl[i].v, L["s5f_d"][:, i, cs])
                    lre, lim = fl[0], fl[1]
                    k.act(fl[2].v, fl[2].v, AF.Exp)
                    k.tt(a[0].v, lre.v, fl[2].v, ALU.mult)
                    k.tt(a[1].v, lim.v, fl[2].v, ALU.mult)
                    k.act(a[0].v, a[0].v, AF.Exp)
                    self.sincos(a[2].v, a[1].v, N, 0.25, tmp)
                    self.sincos(a[3].v, a[1].v, N, 0.0, tmp)
                    k.tt(a[2].v, a[2].v, a[0].v, ALU.mult)
                    k.tt(a[3].v, a[3].v, a[0].v, ALU.mult)
                    k.ts(a[2].v, a[2].v, -1.0, ALU.add)
                    k.tt(a[0].v, lre.v, lre.v, ALU.mult)
                    k.tt(a[1].v, lim.v, lim.v, ALU.mult)
                    k.tt(a[0].v, a[0].v, a[1].v, ALU.add)
                    k.op("dve", "reciprocal", out=a[0].v, in_=a[0].v)
                    k.tt(a[1].v, a[2].v, lre.v, ALU.mult)
                    k.tt(a[4].v, a[3].v, lim.v, ALU.mult)
                    k.tt(a[1].v, a[1].v, a[4].v, ALU.add)
                    k.tt(a[1].v, a[1].v, a[0].v, ALU.mult)
                    k.tt(a[4].v, a[3].v, lre.v, ALU.mult)
                    k.tt(a[5].v, a[2].v, lim.v, ALU.mult)
                    k.tt(a[4].v, a[4].v, a[5].v, ALU.subtract)
                    k.tt(a[4].v, a[4].v, a[0].v, ALU.mult)
                    k.dma(zb[0].v, L["zb_d"][:, 0, s * 4:(s + 1) * 4, :])
                    k.dma(zb[1].v, L["zb_d"][:, 1, s * 4:(s + 1) * 4, :])
                    zr, zi = zb[0].v.re("p t n -> p (t n)"), zb[1].v.re("p t n -> p (t n)")
                    fre, fim = a[1].v, a[4].v
                    k.tt(a[2].v, fre, zr, ALU.mult)
                    k.tt(a[3].v, fim, zi, ALU.mult)
                    k.tt(LBre[:, s * 4:(s + 1) * 4, :].re("p t n -> p (t n)"), a[2].v, a[3].v, ALU.subtract)
                    k.tt(a[2].v, fre, zi, ALU.mult)
                    k.tt(a[3].v, fim, zr, ALU.mult)
                    k.tt(LBim[:, s * 4:(s + 1) * 4, :].re("p t n -> p (t n)"), a[2].v, a[3].v, ALU.add)
                    if B:
                        k.dma(zb[0].v, L["zc_d"][:, 0, s * 4:(s + 1) * 4, :])
                        k.dma(zb[1].v, L["zc_d"][:, 1, s * 4:(s + 1) * 4, :])
                        k.cp(LCre[:, s * 4:(s + 1) * 4, :], zb[0].v)
                        k.ts(LCim[:, s * 4:(s + 1) * 4, :], zb[1].v, -1.0, ALU.mult)
            if os.environ.get("KS5") == "p2":
                return soT
            Wu = k.sb("Wu", [128, 16, 128], BF16, ph)
            uTf = k.sb("uTf", [128, NT], F32, ph)
            uTb = k.sb("uTb", [128, NT], BF16, ph)
            hp = k.sb("hp", [128, 2, 4], F32, ph)
            br = k.sb("br", [128, 128], F32, ph)
            bi = k.sb("bi", [128, 128], F32, ph)
            x1 = k.sb("x1", [128, 128], F32, ph)
            x2 = k.sb("x2", [128, 128], F32, ph)
            gre = k.sb("gre", [128, 128], F32, ph)
            gim = k.sb("gim", [128, 128], F32, ph)
            tc = k.sb("tc", [128, 2], F32, ph)
            if B:
                hre = k.sb("hre", [128, 128], BF16, ph)
                him = k.sb("him", [128, 128], BF16, ph)
                Es = k.sb("Es", [128, NCORE, 64], F32, ph)
                k.dma(Es.v, L["e_s5_d"].v)
                hin = k.sb("hin", [128, 2, 4], F32, ph)
                y = k.sb("y", [128, 128], F32, ph)
                y2 = k.sb("y2", [128, 128], F32, ph)
            else:
                Eo = k.sb("Eo", [128, 64], F32, ph)
            blocks = [(n0, min(384, NT - n0)) for n0 in range(c0 * C, NT, 384)]
            for j in range(8):
                self.load_w(Wu, win, self.o(O_SU) + 128 * j, 128)
                for bi_, (n0, n) in enumerate(blocks):
                    p = pA[bi_ % 2]
                    for kt in range(16):
                        k.mm(p[:, 0:n], Wu[:, kt, :], hnT[:, kt, n0:n0 + n], start=kt == 0, stop=kt == 15)
                    k.act(uTf[:, n0:n0 + n], p[:, 0:n], AF.Identity)
                    k.cp(uTb[:, n0:n0 + n], p[:, 0:n])
                if os.environ.get("KS5") == "m1":
                    return soT
                k.memset(hp.v, 0.0)
                for c in range(c0, NCH):
                    cs = slice(c * C, (c + 1) * C)
                    for q in range(4):
                        t = 4 * j + q
                        k.mm(pB[0][:, 0:128], LBre[:, t, :], uTb[:, cs])
                        k.mm(pB[1][:, 0:128], LBim[:, t, :], uTb[:, cs])
                        k.tt(x1.v, pB[0][:, 0:128], cosT[:, t, :], ALU.mult)
                        k.tt(x2.v, pB[1][:, 0:128], sinT[:, t, :], ALU.mult)
                        k.tt(br.v, x1.v, x2.v, ALU.add)
                        k.tt(x1.v, pB[1][:, 0:128], cosT[:, t, :], ALU.mult)
                        k.tt(x2.v, pB[0][:, 0:128], sinT[:, t, :], ALU.mult)
                        k.tt(bi.v, x1.v, x2.v, ALU.subtract)
                        if os.environ.get("KS5") == "m2":
                            return soT
                        rb_ = rfill[:, t, :]
                        k.op("dve", "tensor_tensor_scan", out=gre.v, data0=rb_, data1=br.v, initial=hp[:, 0, q:q + 1], op0=ALU.mult, op1=ALU.add)
                        k.op("dve", "tensor_tensor_scan", out=gim.v, data0=rb_, data1=bi.v, initial=hp[:, 1, q:q + 1], op0=ALU.mult, op1=ALU.add)
                        if os.environ.get("KS5") == "m3":
                            return soT
                        if (not B) and L.get("dbg_d") is not None and j == 0 and q == 1 and c == c0:
                            dbt = k.sb("dbt", [128, 12, 128], F32, ph)
                            srcs = [cosT[:, t, :], sinT[:, t, :], rfill[:, t, :], LBre[:, t, :], LBim[:, t, :], pB[0][:, 0:128],
                                    pB[1][:, 0:128], br.v, bi.v, gre.v, gim.v, uTf[:, cs]]
                            for di, sv in enumerate(srcs):
                                k.ts(dbt[:, di, :], sv, 1.0, ALU.mult)
                            k.dma(L["dbg_d"].v, dbt.v)
                        c128, s128 = rot[:, 0, t:t + 1], rot[:, 1, t:t + 1]
                        k.ts(tc[:, 0:1], gim[:, 127:128], s128, ALU.mult)
                        k.ts(tc[:, 1:2], gim[:, 127:128], c128, ALU.mult)
                        k.stt(hp[:, 0, q:q + 1], gre[:, 127:128], c128, tc[:, 0:1], ALU.mult, ALU.subtract)
                        k.stt(hp[:, 1, q:q + 1], gre[:, 127:128], s128, tc[:, 1:2], ALU.mult, ALU.add)
                        if B:
                            k.tt(x1.v, gre.v, cosT[:, t, :], ALU.mult)
                            k.tt(x2.v, gim.v, sinT[:, t, :], ALU.mult)
                            k.tt(hre.v, x1.v, x2.v, ALU.subtract)
                            k.tt(x1.v, gre.v, sinT[:, t, :], ALU.mult)
                            k.tt(x2.v, gim.v, cosT[:, t, :], ALU.mult)
                            k.tt(him.v, x1.v, x2.v, ALU.add)
                            k.mm(pB[2][:, 0:128], LCre[:, t, :], hre.v, start=q == 0, stop=False)
                            k.mm(pB[2][:, 0:128], LCim[:, t, :], him.v, start=False, stop=q == 3)
                    if B and c == 0:
                        ts_ = slice(4 * j, 4 * j + 4)
                        k.memset(hin.v, 0.0)
                        for cc in range(NCORE):
                            oh = masks[:, 1 + cc:2 + cc]
                            k.stt(hin[:, 0, :], hp[:, 0, :], oh, hin[:, 0, :], ALU.mult, ALU.add)
                            k.stt(hin[:, 1, :], hp[:, 1, :], oh, hin[:, 1, :], ALU.mult, ALU.add)
                            ar, ai = rot[:, 2, ts_], rot[:, 3, ts_]
                            k.tt(x1[:, 0:4], hp[:, 0, :], ar, ALU.mult)
                            k.tt(x1[:, 4:8], hp[:, 1, :], ai, ALU.mult)
                            k.tt(x1[:, 8:12], hp[:, 0, :], ai, ALU.mult)
                            k.tt(x1[:, 12:16], hp[:, 1, :], ar, ALU.mult)
                            k.tt(x1[:, 0:4], x1[:, 0:4], x1[:, 4:8], ALU.subtract)
                            k.tt(x1[:, 8:12], x1[:, 8:12], x1[:, 12:16], ALU.add)
                            k.tt(hp[:, 0, :], x1[:, 0:4], Es[:, cc, ts_], ALU.add)
                            k.tt(hp[:, 1, :], x1[:, 8:12], Es[:, cc, 32 + 4 * j:32 + 4 * j + 4], ALU.add)
                        k.cp(hp.v, hin.v)
                    if B:
                        k.stt(y.v, uTf[:, cs], s5d[:, j:j + 1], pB[2][:, 0:128], ALU.mult, ALU.add)
                        k.tt(y2.v, y.v, y.v, ALU.mult)
                        k.ts(y2.v, y2.v, 0.044715, ALU.mult, 1.0, ALU.add)
                        k.tt(y2.v, y2.v, y.v, ALU.mult)
                        k.act(y2.v, y2.v, AF.Sigmoid, scale=1.5957691216057308)
                        k.tt(geluT[:, j, cs], y.v, y2.v, ALU.mult)
                if not B:
                    k.cp(Eo[:, 4 * j:4 * j + 4], hp[:, 0, :])
                    k.cp(Eo[:, 32 + 4 * j:32 + 4 * j + 4], hp[:, 1, :])
            if not B:
                k.dma(L["o_s5"].v, Eo.v)
            else:
                Wgl = k.sb("Wgl", [128, 8, 1024], BF16, ph)
                self.load_w(Wgl, L["wglu"], 0, 1024, nk=8)
                Wsg = k.sb("Wsg", [128, 16, 128], BF16, ph)
                sgm = k.sb("sgm", [128, 384], F32, ph)
                sl = k.sb("sl", [128, 384], F32, ph)
                for j in range(8):
                    self.load_w(Wsg, win, O_SG + 128 * j, 128)
                    for (n0, n) in blocks:
                        for kk in range(8):
                            k.mm(pA[0][:, 0:n], Wgl[:, kk, j * 128:(j + 1) * 128], geluT[:, kk, n0:n0 + n], start=kk == 0, stop=kk == 7)
                        for kt in range(16):
                            k.mm(pA[1][:, 0:n], Wsg[:, kt, :], hnT[:, kt, n0:n0 + n], start=kt == 0, stop=kt == 15)
                        k.act(sgm[:, 0:n], pA[0][:, 0:n], AF.Sigmoid)
                        k.act(sl[:, 0:n], pA[1][:, 0:n], AF.Silu)
                        k.tt(sgm[:, 0:n], sgm[:, 0:n], geluT[:, j, n0:n0 + n], ALU.mult)
                        k.tt(soT[:, j, n0:n0 + n], sgm[:, 0:n], sl[:, 0:n], ALU.mult)
        return soT

    def phase_merge(self, L, roT, goT, soT):
        k, hnT, pA, pB, win = self.k, self.hnT, self.pA, self.pB, self.win
        brT = [roT, goT, soT]
        with k.scope() as ph:
            yT = k.sb("yT", [128, 16, NT], BF16, ph)
            Wb = k.sb("Wb", [128, 24, 128], BF16, ph)
            Wg = k.sb("Wgm", [128, 48, 128], BF16, ph)
            sg = k.sb("sgm", [128, 384], F32, ph)
            acc = k.sb("acc", [128, 384], F32, ph)
            tmp = k.sb("tmpm", [128, 384], F32, ph)
            blocks = [(n0, 384) for n0 in range(0, NT, 384)]
            for m in range(16):
                for b in range(3):
                    self.load_w(Wb[:, 8 * b:8 * b + 8, :], L["wbr"][b], m * 128, 128, nk=8)
                    self.load_w(Wg[:, 16 * b:16 * b + 16, :], win, O_MG + b * D + m * 128, 128)
                for (n0, n) in blocks:
                    for b in range(3):
                        pP, pG = pB[b], pA[b % 2]
                        for kk in range(8):
                            k.mm(pP[:, 0:n], Wb[:, 8 * b + kk, :], brT[b][:, kk, n0:n0 + n], start=kk == 0, stop=kk == 7)
                        for kt in range(16):
                            k.mm(pG[:, 0:n], Wg[:, 16 * b + kt, :], hnT[:, kt, n0:n0 + n], start=kt == 0, stop=kt == 15)
                        k.act(sg.v, pG[:, 0:n], AF.Sigmoid)
                        if b == 0:
                            k.tt(acc.v, sg.v, pP[:, 0:n], ALU.mult)
                        elif b == 1:
                            k.tt(tmp.v, sg.v, pP[:, 0:n], ALU.mult)
                            k.tt(acc.v, acc.v, tmp.v, ALU.add)
                        else:
                            k.tt(tmp.v, sg.v, pP[:, 0:n], ALU.mult)
                            k.tt(yT[:, m, n0:n0 + n], acc.v, tmp.v, ALU.add)
            Wo = k.sb("Wo", [128, 16, 512], BF16, ph)
            hres = [k.sb(f"hres{i}", [128, 512], F32, ph) for i in range(2)]
            hout_d, h_d = L["hout_d"], L["h_d"]
            tgt = hout_d if not self.final else self.k.dram("h_tmp", [NT, D], F32, "Internal")
            for nb in range(4):
                self.load_w(Wo, L["wout"], nb * 512, 512)
                for c in range(NCH):
                    p = pA[c % 2]
                    for m in range(16):
                        k.mm(p[:, 0:512], yT[:, m, c * C:(c + 1) * C], Wo[:, m, :], start=m == 0, stop=m == 15)
                    hr = hres[c % 2]
                    k.dma(hr.v, h_d[c * C:(c + 1) * C, nb * 512:(nb + 1) * 512])
                    k.tt(hr.v, hr.v, p[:, 0:512], ALU.add)
                    k.dma(tgt[c * C:(c + 1) * C, nb * 512:(nb + 1) * 512], hr.v)
        if self.final:
            ph = self.es
            ht = [k.sb(f"htf{i}", [128, D], F32, ph) for i in range(2)]
            junk = k.sb("junkf", [128, D], F32, ph)
            fg = k.sb("fg", [128, D], F32, ph)
            ss = k.sb("ssf", [128, 2], F32, ph)
            k.dma(fg.v, L["fg_d"].v)
            for c in range(NCH):
                t = ht[c % 2]
                k.dma(t.v, tgt[c * C:(c + 1) * C, :])
                k.act(junk.v, t.v, AF.Square, accum_out=ss[:, 0:1])
                k.act(ss[:, 1:2], ss[:, 0:1], AF.Sqrt, scale=1.0 / D, bias=EPS)
                k.op("dve", "reciprocal", out=ss[:, 1:2], in_=ss[:, 1:2])
                k.act(t.v, t.v, AF.Identity, scale=ss[:, 1:2])
                k.tt(t.v, t.v, fg.v, ALU.mult)
                k.dma(hout_d[c * C:(c + 1) * C, :], t.v)


_PROGS = {}


def _prog(mode, final):
    key = (mode, final)
    if key not in _PROGS:
        p = Prog(mode, final)
        p.build()
        _PROGS[key] = p
    return _PROGS[key]


def _consts(core):
    f = np.float32
    gam = (1.0 - 2.0 ** (-5.0 - np.arange(8))).astype(np.float64)
    idx = np.arange(128)
    rel = idx[None, :] - idx[:, None]
    dm = np.where(rel >= 0, gam[:, None, None] ** np.maximum(rel, 0)[None], 0.0)
    out = {
        "dmatT": np.ascontiguousarray(dm.transpose(1, 0, 2)).astype(f),
        "kdec": (gam[None, :] ** (127 - idx)[:, None]).astype(f),
        "qdecB": np.broadcast_to((gam[:, None] ** (idx + 1)[None, :])[None], (128, 8, 128)).astype(f).copy(),
        "tri": np.stack([(rel >= 0), (rel < 0), np.ones((128, 128), bool)], 1).astype(f),
        "identF": np.eye(128, dtype=f),
        "iota": np.broadcast_to((idx + 1).astype(f)[None], (128, 128)).copy(),
    }
    m = np.zeros((128, 16), f)
    m[112:, 0] = 1.0
    m[:, 1 + core] = 1.0
    m[:, 9] = 1.0
    out["masks"] = m
    pos = np.zeros((128, NCH), f)
    pos[112:, 0] = np.arange(16)
    for c in range(1, NCH):
        pos[:, c] = 16 + core * 1024 + (c - 1) * 128 + idx
    inv = (np.float32(10000.0) ** (-np.arange(64, dtype=f) / np.float32(64))).astype(f)
    ang = (pos[:, :, None] * inv[None, None, :]).astype(f)
    rq = np.stack([np.cos(ang), np.sin(ang)], 2).astype(f)
    out["rot_q"] = rq
    out["rot_k"] = (rq * f(128 ** -0.5)).astype(f)
    return out


def _layer_inputs(inp, l):
    f = np.float32
    d = {}
    d["norm_gT"] = np.ascontiguousarray(inp["norm_g"][l].reshape(16, 128).T)
    d["w_in"] = inp["w_in"][l]
    d["gla_w_a"] = inp["gla_w_a"][l]
    d["gla_b_a"] = np.broadcast_to(inp["gla_b_a"][l].reshape(1, 512), (128, 512))
    lre, lim, ldt = inp["s5_lambda_re"][l], inp["s5_lambda_im"][l], inp["s5_log_dt"][l]
    toP = lambda x: np.ascontiguousarray(x.reshape(32, 2, 64).transpose(1, 2, 0).reshape(128, 32))
    ldtP = np.repeat(ldt.reshape(32, 2).T[:, None, :], 64, axis=1).reshape(128, 32)
    d["s5_P"] = np.ascontiguousarray(np.stack([toP(lre), toP(lim), ldtP], 1)).astype(f)
    d["s5_F"] = np.broadcast_to(np.stack([lre.reshape(4096), lim.reshape(4096), np.repeat(ldt, 64)], 0)[None], (128, 3, 4096)).astype(f)
    LB = np.zeros((128, 2, 32, 128), f)
    LC = np.zeros((128, 2, 32, 128), f)
    for t in range(32):
        q = t % 4
        for g2 in range(2):
            g = 2 * t + g2
            r0 = 32 * q + 16 * g2
            for i, (bb, cc) in enumerate(((inp["s5_b_re"], inp["s5_c_re"]), (inp["s5_b_im"], inp["s5_c_im"]))):
                LB[r0:r0 + 16, i, t, g2 * 64:(g2 + 1) * 64] = bb[l, g].T
                LC[g2 * 64:(g2 + 1) * 64, i, t, r0:r0 + 16] = cc[l, g].T
    d["s5_LB"], d["s5_LC"] = LB, LC
    d["ret_norm_g"] = np.broadcast_to(inp["ret_norm_g"][l].reshape(1, 1024), (128, 1024))
    d["gla_norm_g"] = np.broadcast_to(inp["gla_norm_g"][l].reshape(1, 256), (128, 256))
    d["s5_dT"] = np.ascontiguousarray(inp["s5_d"][l].reshape(8, 128).T)
    d["s5_w_glu"] = inp["s5_w_glu"][l]
    d["w_br_ret"], d["w_br_gla"], d["w_br_s5"] = inp["w_br_ret"][l], inp["w_br_gla"][l], inp["w_br_s5"][l]
    d["w_out"] = inp["w_out"][l]
    d["final_g"] = np.broadcast_to(inp["final_g"].reshape(1, D), (128, D))
    return {k_: np.ascontiguousarray(v, dtype=f) for k_, v in d.items()}


def _run(prog, per_core):
    names = list(prog.inputs.keys())
    maps = [{n: m[n] for n in names} for m in per_core]
    res = run_bass_kernel_spmd(prog.nc, maps, core_ids=list(range(NCORE)))
    return res.results


def kernel(**inp):
    inp = {k_: np.asarray(v) for k_, v in inp.items()}
    x = inp["x"].reshape(8192, D).astype(np.float32)
    meta = inp["meta"].astype(np.float32)
    consts = [_consts(c) for c in range(NCORE)]
    hs = []
    for c in range(NCORE):
        h = np.zeros((NT, D), np.float32)
        h[112:128] = meta
        h[128:] = x[c * 1024:(c + 1) * 1024]
        hs.append(h)
    depth = inp["w_in"].shape[0]
    for l in range(depth):
        lw = _layer_inputs(inp, l)
        pa = _prog("A", False)
        wa = np.ascontiguousarray(np.concatenate([lw["w_in"][:, b:b + n] for b, n in Prog.A_COLS], 1))
        ra = _run(pa, [dict(lw, **consts[c], h=hs[c], w_in=wa) for c in range(NCORE)])
        gath = {n: np.ascontiguousarray(np.stack([ra[c][n] for c in range(NCORE)], 1)) for n in ("E_ret", "E_gla", "D_gla", "E_s5")}
        pb = _prog("B", False)
        rb = _run(pb, [dict(lw, **consts[c], **gath, h=hs[c]) for c in range(NCORE)])
        hs = [rb[c]["h_out"] for c in range(NCORE)]
    pf = _prog("F", False)
    fg = np.ascontiguousarray(np.broadcast_to(inp["final_g"].reshape(1, D), (128, D)), dtype=np.float32)
    rf = _run(pf, [dict(h=hs[c], final_g=fg) for c in range(NCORE)])
    hs = [rf[c]["h_out"] for c in range(NCORE)]
    out = np.concatenate([h[128:] for h in hs], 0).reshape(1, 8192, D)
    return out.astype(np.float32)
```
